# Optimizing a Trainium2 kernel written in Bass

```python
import math
import jax, jax.numpy as jnp
from jax import lax
import numpy as np


D_MODEL = 1024
BATCH = 8
SEQ = 4096
DEPTH = 4

D_ATTN = D_MODEL // 2
HEAD_DIM = 64
N_HEADS = D_ATTN // HEAD_DIM
ROPE_DIM = HEAD_DIM // 4
ROPE_THETA = 500000.0
DILATED_PATTERNS = ((128, 1), (512, 4), (2048, 16))
D_SSM = D_MODEL - D_ATTN
SSM_GROUP = 16
N_SSM_GROUPS = D_SSM // SSM_GROUP
SSM_STATE = 64
DT_MIN = 0.001
DT_MAX = 0.1
D_MIX = D_ATTN + D_SSM
D_IN_PROJ = 3 * D_ATTN + D_SSM
D_FF = 128 * (-(-(8 * D_MODEL // 3) // 128))
PLE_DIM = 256
NORM_EPS = 1e-6

kernel_name = 'hybrid_s5_dilated_macaron_block'


def rms_norm(x, g):
    xf = x.astype(jnp.float32)
    y = xf * lax.rsqrt(jnp.mean(xf * xf, axis=-1, keepdims=True) + NORM_EPS)
    return (y * g.astype(jnp.float32)).astype(x.dtype)


def swiglu(x, w_gate, w_up, w_down):
    return (jax.nn.silu(x @ w_gate) * (x @ w_up)) @ w_down


def partial_rotary(t, positions):
    half = ROPE_DIM // 2
    inv_freq = ROPE_THETA ** (-jnp.arange(half, dtype=jnp.float32) * (2.0 / ROPE_DIM))
    ang = positions.astype(jnp.float32)[:, :, None, None] * inv_freq
    cos, sin = jnp.cos(ang), jnp.sin(ang)
    tf = t.astype(jnp.float32)
    t1, t2, rest = tf[..., :half], tf[..., half:ROPE_DIM], tf[..., ROPE_DIM:]
    out = jnp.concatenate([t1 * cos - t2 * sin, t2 * cos + t1 * sin, rest], axis=-1)
    return out.astype(t.dtype)


def dilated_band_attention(q, k, v, window, dilation):
    b_, s_, h_, dh = q.shape
    band = window // dilation
    n_str = s_ // dilation
    nb = -(-n_str // band)
    lp = nb * band
    nrows = b_ * dilation

    def to_blocks(t):
        t = t.reshape(b_, n_str, dilation, h_, dh).transpose(0, 2, 1, 3, 4).reshape(nrows, n_str, h_, dh)
        t = jnp.pad(t, ((0, 0), (0, lp - n_str), (0, 0), (0, 0)))
        return t.reshape(nrows, nb, band, h_, dh)

    def with_prev(t):
        prev = jnp.pad(t[:, :-1], ((0, 0), (1, 0), (0, 0), (0, 0), (0, 0)))
        return jnp.concatenate([prev, t], axis=2)

    qb = to_blocks(q)
    kc = with_prev(to_blocks(k))
    vc = with_prev(to_blocks(v))
    scores = jnp.einsum('nbqhd,nbkhd->nbhqk', qb, kc).astype(jnp.float32) * (HEAD_DIM ** -0.5)
    qi = jnp.arange(band)[:, None]
    kj = jnp.arange(2 * band)[None, :]
    dist = qi + band - kj
    band_ok = (dist >= 0) & (dist <= band)
    blk = jnp.arange(nb)[:, None, None]
    mask = band_ok[None] & ((blk > 0) | (kj >= band)[None])
    scores = jnp.where(mask[None, :, None], scores, -jnp.inf)
    m = jnp.max(scores, axis=-1, keepdims=True)
    e = jnp.exp(scores - m)
    den = jnp.sum(e, axis=-1, keepdims=True)
    probs = (e / den).astype(v.dtype)
    out = jnp.einsum('nbhqk,nbkhd->nbqhd', probs, vc).astype(jnp.float32)
    lse = (m + jnp.log(den))[..., 0]
    out = out.reshape(nrows, lp, h_, dh)[:, :n_str]
    out = out.reshape(b_, dilation, n_str, h_, dh).transpose(0, 2, 1, 3, 4).reshape(b_, s_, h_, dh)
    lse = lse.transpose(0, 1, 3, 2).reshape(nrows, lp, h_)[:, :n_str]
    lse = lse.reshape(b_, dilation, n_str, h_).transpose(0, 2, 1, 3).reshape(b_, s_, h_)
    return out, lse


def dilated_mixture_attention(q, k, v):
    outs, lses = [], []
    for window, dilation in DILATED_PATTERNS:
        o, l = dilated_band_attention(q, k, v, window, dilation)
        outs.append(o)
        lses.append(l)
    wts = jax.nn.softmax(jnp.stack(lses, axis=0), axis=0)
    out = jnp.sum(wts[..., None] * jnp.stack(outs, axis=0), axis=0)
    b_, s_ = q.shape[0], q.shape[1]
    return out.reshape(b_, s_, D_ATTN).astype(q.dtype)


def s5_mixer(u, lam_re, lam_im, log_dt, b_re, b_im, c_re, c_im, d_skip, w_glu, b_glu):
    b_, s_ = u.shape[0], u.shape[1]
    uf = u.astype(jnp.float32).reshape(b_, s_, N_SSM_GROUPS, SSM_GROUP)
    lam = lax.complex(lam_re.astype(jnp.float32), lam_im.astype(jnp.float32))
    dt = jnp.exp(log_dt.astype(jnp.float32))[:, None]
    lam_bar = jnp.exp(lam * dt)
    b_mat = lax.complex(b_re.astype(jnp.float32), b_im.astype(jnp.float32))
    b_bar = ((lam_bar - 1.0) / lam)[..., None] * b_mat
    bu = jnp.einsum('bsgh,gph->bsgp', uf.astype(jnp.complex64), b_bar)
    a = jnp.broadcast_to(lam_bar, bu.shape)

    def combine(left, right):
        a_l, x_l = left
        a_r, x_r = right
        return a_r * a_l, a_r * x_l + x_r

    _, states = lax.associative_scan(combine, (a, bu), axis=1)
    c_mat = lax.complex(c_re.astype(jnp.float32), c_im.astype(jnp.float32))
    y = jnp.real(jnp.einsum('bsgp,ghp->bsgh', states, c_mat)) + d_skip.astype(jnp.float32) * uf
    y = jax.nn.gelu(y).reshape(b_, s_, D_SSM)
    y = y * jax.nn.sigmoid(y @ w_glu.astype(jnp.float32) + b_glu.astype(jnp.float32))
    return y.astype(u.dtype)


def setup_inputs(seed: int = 0) -> dict:
    key = jax.random.key(seed)
    ks = jax.random.split(key, 40)
    f32 = jnp.float32

    def nrm(k, shape, fan_in):
        return jax.random.normal(k, shape, f32) * (fan_in ** -0.5)

    def gain(k, shape):
        return 1.0 + 0.05 * jax.random.normal(k, shape, f32)

    lam_im_base = jnp.pi * jnp.arange(SSM_STATE, dtype=f32)
    return {
        'x': jax.random.normal(ks[0], (BATCH, SEQ, D_MODEL), f32),
        'p': jax.random.normal(ks[1], (DEPTH, BATCH, SEQ, PLE_DIM), f32),
        'positions': (jax.random.randint(ks[2], (BATCH, 1), 0, 1024, dtype=jnp.int32)
                      + jnp.arange(SEQ, dtype=jnp.int32)[None, :]),
        'ffn1_pre_g': gain(ks[3], (DEPTH, D_MODEL)),
        'ffn1_w_gate': nrm(ks[4], (DEPTH, D_MODEL, D_FF), D_MODEL),
        'ffn1_w_up': nrm(ks[5], (DEPTH, D_MODEL, D_FF), D_MODEL),
        'ffn1_w_down': nrm(ks[6], (DEPTH, D_FF, D_MODEL), D_FF),
        'ffn1_post_g': gain(ks[7], (DEPTH, D_MODEL)),
        'mix_pre_g': gain(ks[8], (DEPTH, D_MODEL)),
        'w_in': nrm(ks[9], (DEPTH, D_MODEL, D_IN_PROJ), D_MODEL),
        'attn_norm_g': gain(ks[10], (DEPTH, D_ATTN)),
        'ssm_lam_re': -0.5 + 0.01 * jax.random.normal(ks[11], (DEPTH, N_SSM_GROUPS, SSM_STATE), f32),
        'ssm_lam_im': lam_im_base + 0.01 * jax.random.normal(ks[12], (DEPTH, N_SSM_GROUPS, SSM_STATE), f32),
        'ssm_log_dt': jax.random.uniform(ks[13], (DEPTH, N_SSM_GROUPS), f32,
                                         minval=math.log(DT_MIN), maxval=math.log(DT_MAX)),
        'ssm_b_re': nrm(ks[14], (DEPTH, N_SSM_GROUPS, SSM_STATE, SSM_GROUP), 2 * SSM_GROUP),
        'ssm_b_im': nrm(ks[15], (DEPTH, N_SSM_GROUPS, SSM_STATE, SSM_GROUP), 2 * SSM_GROUP),
        'ssm_c_re': nrm(ks[16], (DEPTH, N_SSM_GROUPS, SSM_GROUP, SSM_STATE), 2 * SSM_STATE),
        'ssm_c_im': nrm(ks[17], (DEPTH, N_SSM_GROUPS, SSM_GROUP, SSM_STATE), 2 * SSM_STATE),
        'ssm_d': jax.random.normal(ks[18], (DEPTH, N_SSM_GROUPS, SSM_GROUP), f32),
        'ssm_w_glu': nrm(ks[19], (DEPTH, D_SSM, D_SSM), D_SSM),
        'ssm_b_glu': 0.01 * jax.random.normal(ks[20], (DEPTH, D_SSM), f32),
        'ssm_norm_g': gain(ks[21], (DEPTH, D_SSM)),
        'w_out': nrm(ks[22], (DEPTH, D_MIX, D_MODEL), D_MIX),
        'mix_post_g': gain(ks[23], (DEPTH, D_MODEL)),
        'ffn2_pre_g': gain(ks[24], (DEPTH, D_MODEL)),
        'ffn2_w_gate': nrm(ks[25], (DEPTH, D_MODEL, D_FF), D_MODEL),
        'ffn2_w_up': nrm(ks[26], (DEPTH, D_MODEL, D_FF), D_MODEL),
        'ffn2_w_down': nrm(ks[27], (DEPTH, D_FF, D_MODEL), D_FF),
        'ffn2_post_g': gain(ks[28], (DEPTH, D_MODEL)),
        'ple_w_up': nrm(ks[29], (DEPTH, PLE_DIM, D_MODEL), PLE_DIM),
        'ple_w_gate': nrm(ks[30], (DEPTH, D_MODEL, D_MODEL), D_MODEL),
        'ple_post_g': gain(ks[31], (DEPTH, D_MODEL)),
    }


def reference(x, p, positions,
              ffn1_pre_g, ffn1_w_gate, ffn1_w_up, ffn1_w_down, ffn1_post_g,
              mix_pre_g, w_in, attn_norm_g,
              ssm_lam_re, ssm_lam_im, ssm_log_dt, ssm_b_re, ssm_b_im, ssm_c_re, ssm_c_im,
              ssm_d, ssm_w_glu, ssm_b_glu, ssm_norm_g, w_out, mix_post_g,
              ffn2_pre_g, ffn2_w_gate, ffn2_w_up, ffn2_w_down, ffn2_post_g,
              ple_w_up, ple_w_gate, ple_post_g):
    b_, s_ = x.shape[0], x.shape[1]
    h = x
    for i in range(DEPTH):
        f = swiglu(rms_norm(h, ffn1_pre_g[i]), ffn1_w_gate[i], ffn1_w_up[i], ffn1_w_down[i])
        h = h + 0.5 * rms_norm(f, ffn1_post_g[i])

        a_in = rms_norm(h, mix_pre_g[i])
        proj = a_in @ w_in[i]
        q, k, v, u = jnp.split(proj, [D_ATTN, 2 * D_ATTN, 3 * D_ATTN], axis=-1)
        q = partial_rotary(q.reshape(b_, s_, N_HEADS, HEAD_DIM), positions)
        k = partial_rotary(k.reshape(b_, s_, N_HEADS, HEAD_DIM), positions)
        v = v.reshape(b_, s_, N_HEADS, HEAD_DIM)
        attn = dilated_mixture_attention(q, k, v)
        ssm = s5_mixer(u, ssm_lam_re[i], ssm_lam_im[i], ssm_log_dt[i], ssm_b_re[i], ssm_b_im[i],
                       ssm_c_re[i], ssm_c_im[i], ssm_d[i], ssm_w_glu[i], ssm_b_glu[i])
        mixed = jnp.concatenate([rms_norm(attn, attn_norm_g[i]), rms_norm(ssm, ssm_norm_g[i])], axis=-1)
        h = h + rms_norm(mixed @ w_out[i], mix_post_g[i])

        f = swiglu(rms_norm(h, ffn2_pre_g[i]), ffn2_w_gate[i], ffn2_w_up[i], ffn2_w_down[i])
        h = h + 0.5 * rms_norm(f, ffn2_post_g[i])

        ple = (p[i] @ ple_w_up[i]) * jax.nn.sigmoid(h @ ple_w_gate[i])
        h = h + rms_norm(ple, ple_post_g[i])
    return h
```

```python
import math
from contextlib import ExitStack
import numpy as np
import concourse.bass as bass
import concourse.mybir as mybir
from concourse.bass_utils import run_bass_kernel_spmd

F32 = mybir.dt.float32
import os as _osx
F32R = mybir.dt.float32 if _osx.environ.get('KF32', '0') == '1' else mybir.dt.float32r
KSQ = _osx.environ.get('KSQ', '0') == '1'
BF16 = mybir.dt.bfloat16
I32 = mybir.dt.int32
AF = mybir.ActivationFunctionType
ALU = mybir.AluOpType

S = 4096
D = 1024
KC = 8
DFF = 2816
FC = 22
NL = 4
EPS = 1e-6
TWO_PI = 2.0 * math.pi
SIN_SCALE = 6.28318
ENG = ["sp", "act", "pe", "dve", "pool"]


class Prog:
    def __init__(self, nc, stack):
        self.nc = nc
        self.stack = stack
        self.ops = {e: [] for e in ENG}
        self.sems = {}
        self.count = {}
        self.seen = {e: {} for e in ENG}
        self.lastw = {}
        self.reads = {}
        for e in ENG[1:]:
            self._sem("e_" + e)

    def _sem(self, name):
        if name not in self.sems:
            self.sems[name] = self.stack.enter_context(self.nc.semaphore(name))
            self.count[name] = 0
        return name

    def op(self, eng, fn, reads=(), writes=(), group=None, final=True):
        need = {}

        def add(sv):
            if sv is None:
                return
            s, v = sv
            if s.startswith("d_"):
                v = self.count[s]
            if need.get(s, 0) < v:
                need[s] = v

        for k in reads:
            add(self.lastw.get(k))
        for k in writes:
            add(self.lastw.get(k))
            for s, v in self.reads.get(k, {}).items():
                add((s, v))
        own = "e_" + eng
        waits = []
        for s, v in need.items():
            if eng == "pe" and s == own:
                continue
            if self.seen[eng].get(s, 0) >= v:
                continue
            self.seen[eng][s] = v
            waits.append((s, v))
        if eng == "sp":
            sname = self._sem("d_" + (group or "misc"))
            inc = 16
        else:
            sname = own
            inc = 1
        if final:
            self.count[sname] += inc
            val = self.count[sname]
        else:
            val = self.count[sname] + inc
        for k in reads:
            self.reads.setdefault(k, {})
            if self.reads[k].get(sname, 0) < val:
                self.reads[k][sname] = val
        for k in writes:
            self.lastw[k] = (sname, val)
            self.reads[k] = {}
        self.ops[eng].append((waits, fn, sname if final else None, inc))

    def barrier(self):
        tot = dict(self.count)
        for e in ENG:
            waits = []
            for s, v in tot.items():
                if v > 0 and self.seen[e].get(s, 0) < v and s != "e_" + e:
                    self.seen[e][s] = v
                    waits.append((s, v))
            if waits:
                self.ops[e].append((waits, None, None, 0))
        self.lastw = {}
        self.reads = {}

    def emit(self, final_wait=False):
        nc = self.nc
        ops = self.ops
        sems = self.sems

        def run(eng_name):
            def body(eng):
                for waits, fn, sname, inc in ops[eng_name]:
                    for s, v in waits:
                        eng.wait_ge(sems[s], v)
                    if fn is not None:
                        ins = fn(eng)
                        if sname is not None:
                            ins.then_inc(sems[sname], inc)
                if final_wait and eng_name == "sp":
                    for s, v in self.count.items():
                        if v > 0:
                            eng.wait_ge(sems[s], v)
            return body

        with nc.Block() as block:
            block.sync(run("sp"))
            block.scalar(run("act"))
            block.tensor(run("pe"))
            block.vector(run("dve"))
            block.gpsimd(run("pool"))
        self.ops = {e: [] for e in ENG}


def fap(t, dims, off=0):
    base = t.ap()
    return bass.AP(base.tensor, base.offset + off, [list(base.ap[0])] + [list(d) for d in dims])


def slab_layout(w):
    L, K, N = w.shape
    a = w.reshape(L, K // 128, 128, N // 128, 128).transpose(0, 3, 2, 1, 4)
    return np.ascontiguousarray(a.reshape(L, N // 128, 128, (K // 128) * 128))


def build_program(n_layers=NL, nlw=NL, phasetest=False):
    nc = bass.Bass("TRN2", target_bir_lowering=False)
    st = ExitStack()
    P = Prog(nc, st)

    declared = {}

    class Lazy:
        def __init__(self, name, shape, dt):
            self.name, self.shape, self.dt, self._ap = name, list(shape), dt, None

        def get(self):
            if self._ap is None:
                self._ap = nc.dram_tensor(self.name, self.shape, self.dt, kind="ExternalInput").ap()
                declared[self.name] = self.shape
            return self._ap

        def __getitem__(self, k):
            return self.get()[k]

        def rearrange(self, *a, **kw):
            return self.get().rearrange(*a, **kw)

        def partition_broadcast(self, n):
            return self.get().partition_broadcast(n)

    def din(name, shape, dt=F32):
        shape = list(shape)
        if shape and shape[0] == NL and name not in ("pT",):
            shape[0] = nlw
        if name == "pT":
            shape[0] = nlw
        return Lazy(name, shape, dt)

    xT = din("xT", [D, S])
    pT = din("pT", [NL, 256, S])
    pos = din("pos", [1, S], I32)
    wg = [din("wg1", [NL, FC, 128, D]), din("wg2", [NL, FC, 128, D])]
    wu = [din("wu1", [NL, FC, 128, D]), din("wu2", [NL, FC, 128, D])]
    wd = [din("wd1", [NL, FC, 128, D]), din("wd2", [NL, FC, 128, D])]
    win = din("win", [NL, 16, 128, D])
    wout = din("wout", [NL, 8, 128, D])
    wglu = din("wglu", [NL, 4, 128, 512])
    plewu = din("plewu", [NL, 8, 128, 256])
    plewg = din("plewg", [NL, 8, 128, D])
    gains = din("gains", [128, NL * 7 * 8])
    gains2 = din("gains2", [128, NL * 2 * 4])
    bglu = din("bglu", [128, NL * 4])
    s_lr = din("s_lr", [NL, 128, 32])
    s_li = din("s_li", [NL, 128, 32])
    s_ld = din("s_ld", [NL, 128, 32])
    s_bx1 = din("s_bx1", [NL, 128, 512])
    s_bx2 = din("s_bx2", [NL, 128, 512])
    s_cy1 = din("s_cy1", [NL, 128, 512])
    s_cy2 = din("s_cy2", [NL, 128, 512])
    s_dst = din("s_dst", [NL, 128, 32])
    c_ident = din("c_ident", [128, 128])
    c_causal = din("c_causal", [128, 128])
    c_mask = din("c_mask", [128, 256])
    c_shift = din("c_shift", [128, 64])
    c_vec = din("c_vec", [128, 8])
    c_expo = din("c_expo", [128, 24])
    out = nc.dram_tensor("out", [D, S], F32, kind="ExternalOutput").ap()

    import os as _os2
    KDEBUG = _os2.environ.get("KDEBUG", "1") == "1"

    def dscr(name, shape):
        if KDEBUG:
            return nc.dram_tensor(name, list(shape), F32, kind="ExternalOutput").ap()
        return nc.dram_tensor(name, list(shape), F32).ap()

    qkvT = dscr("qkvT", [1536, S])
    uP = dscr("uP", [512, 8, 512])
    ysc = dscr("ysc", [512, 8, 512])
    attnT = dscr("attnT", [512, S])
    ropeC = dscr("ropeC", [128, S])
    ropeS = dscr("ropeS", [128, S])
    hT = out
    hTv = hT.rearrange("(kc p) t -> p kc t", p=128)
    attnT_src, ysc_src = attnT, ysc

    class PTile:
        def __init__(self, name, h):
            self.name = name
            self.h = h

        def ap(self):
            return self.h.ap()

    ps = [PTile(f"ps{i}", nc.alloc_psum_tensor(f"ps{i}", [128, 512], F32)) for i in range(8)]

    uniq = [0]

    class Tile:
        def __init__(self, name, h):
            self.name = name
            self.h = h

        def ap(self):
            return self.h.ap()

    def T(name, shape, dt=F32):
        uniq[0] += 1
        return Tile(name, st_ph.enter_context(nc.sbuf_tensor(f"{name}_{uniq[0]}", list(shape), dt)))

    def dma(o, i, reads, writes, group):
        if isinstance(i, Lazy):
            i = i.get()
        P.op("sp", lambda e: e.dma_start(out=o, in_=i), reads, writes, group=group)

    def mm(o, lhsT, rhs, start, stop, reads, writes, final=None):
        P.op("pe", lambda e: e.matmul(o, lhsT, rhs, start=start, stop=stop), reads, writes, final=True)

    def transp(o, i, ident, reads, writes):
        P.op("pe", lambda e: e.transpose(o, i, ident), reads, writes)

    def act(o, i, func, reads, writes, scale=1.0, bias=0.0):
        P.op("act", lambda e: e.activation(out=o, in_=i, func=func, bias=bias, scale=scale), reads, writes)

    def tt(eng, o, a, b, op, reads, writes):
        P.op(eng, lambda e: e.tensor_tensor(out=o, in0=a, in1=b, op=op), reads, writes)

    def ts(eng, o, a, s1, s2, op0, op1, reads, writes):
        P.op(eng, lambda e: e.tensor_scalar(out=o, in0=a, scalar1=s1, scalar2=s2, op0=op0, op1=op1), reads, writes)

    def stt(o, a, sc, b, op0, op1, reads, writes):
        P.op("dve", lambda e: e.scalar_tensor_tensor(out=o, in0=a, scalar=sc, in1=b, op0=op0, op1=op1), reads, writes)

    def cp(eng, o, i, reads, writes):
        if eng == "act":
            P.op("act", lambda e: e.copy(out=o, in_=i), reads, writes)
        else:
            P.op(eng, lambda e: e.tensor_copy(out=o, in_=i), reads, writes)

    def recip(o, i, reads, writes):
        P.op("dve", lambda e: e.reciprocal(out=o, in_=i), reads, writes)

    def range_reduce(eng, y, yi, tmp, k):
        cp(eng, yi, y, [k], [k + "i"])
        cp(eng, tmp, yi, [k + "i"], [k + "t"])
        tt(eng, y, y, tmp, ALU.subtract, [k, k + "t"], [k])
        ts(eng, tmp, y, 0.5, None, ALU.is_gt, ALU.bypass, [k], [k + "t"])
        tt(eng, y, y, tmp, ALU.subtract, [k, k + "t"], [k])
        ts(eng, tmp, y, -0.5, None, ALU.is_lt, ALU.bypass, [k], [k + "t"])
        tt(eng, y, y, tmp, ALU.add, [k, k + "t"], [k])
        ts(eng, y, y, 0.5, -0.5, ALU.min, ALU.max, [k], [k])

    st_all = ExitStack()
    st_ph = st_all
    ident = T("ident", [128, 128])
    onesr = T("onesr", [128, 128], F32R)
    cvec = T("cvec", [128, 8])
    gn = T("gn", [128, NL * 7 * 8])
    gnh = T("gnh", [128, NL * 7 * 8])
    gn2 = T("gn2", [128, NL * 2 * 4])
    bgl = T("bgl", [128, NL * 4])
    dma(ident.ap(), c_ident, [], ["ident"], "c0")
    dma(cvec.ap(), c_vec, [], ["cvec"], "c1")
    dma(gn.ap(), gains, [], ["gn"], "c2")
    dma(gn2.ap(), gains2, [], ["gn2"], "c3")
    dma(bgl.ap(), bglu, [], ["bgl"], "c4")
    onesf = T("onesf", [128, 128])
    P.op("dve", lambda e: e.memset(onesf.ap(), 1.0), [], ["onesf"])
    cp("dve", onesr.ap(), onesf.ap(), ["onesf"], ["onesr"])
    ts("dve", gnh.ap(), gn.ap(), 0.5, None, ALU.mult, ALU.bypass, ["gn"], ["gnh"])
    import os as _os
    K0 = int(_os.environ.get("K0", "99"))
    if K0 >= 2:
        for kc in range(KC):
            dma(hTv[:, kc, :], xT.rearrange("(kc p) t -> p kc t", p=128)[:, kc, :], [], ["hT"], "h0")

    def gcol(l, which, kc, half=False):
        c = (l * 7 + which) * 8 + kc
        return (gnh if half else gn).ap()[:, c:c + 1]

    with ExitStack() as st_ph:
        posi = T("posi", [128, S], I32)
        y1 = T("ry1", [128, S])
        y2 = T("ry2", [128, S])
        yi = T("ryi", [128, S], I32)
        tmp = T("rtmp", [128, S])
        if K0 >= 3:
          dma(posi.ap(), pos.partition_broadcast(128), [], ["posi"], "r0")
          cp("dve", y1.ap(), posi.ap(), ["posi"], ["y1"])
        if K0 >= 3:
          ts("dve", y1.ap(), y1.ap(), cvec.ap()[:, 2:3], None, ALU.mult, ALU.bypass, ["y1", "cvec"], ["y1"])
          ts("dve", y1.ap(), y1.ap(), 1.0 / TWO_PI, None, ALU.mult, ALU.bypass, ["y1"], ["y1"])
          ts("pool", y2.ap(), y1.ap(), 0.25, None, ALU.add, ALU.bypass, ["y1"], ["y2"])
        if K0 >= 4:
          range_reduce("dve", y1.ap(), yi.ap(), tmp.ap(), "y1")
          act(y1.ap(), y1.ap(), AF.Sin, ["y1"], ["y1"], scale=SIN_SCALE)
          ts("dve", y1.ap(), y1.ap(), cvec.ap()[:, 3:4], None, ALU.mult, ALU.bypass, ["y1", "cvec"], ["y1"])
          dma(ropeS, y1.ap(), ["y1"], ["ropeS"], "r1")
        if K0 >= 5:
          range_reduce("dve", y2.ap(), yi.ap(), tmp.ap(), "y2")
          act(y2.ap(), y2.ap(), AF.Sin, ["y2"], ["y2"], scale=SIN_SCALE)
          dma(ropeC, y2.ap(), ["y2"], ["ropeC"], "r2")
        P.barrier()
        P.emit()

    def sqrt_eps(o, i, dim, reads, writes):
        if KSQ:
            ts("dve", o, i, 1.0 / dim, EPS, ALU.mult, ALU.add, reads, writes)
            act(o, o, AF.Sqrt, writes, writes)
        else:
            act(o, i, AF.Sqrt, reads, writes, scale=1.0 / dim, bias=EPS)

    def rms_rstd(src_fn, nk, dim, sq, stat_ps, tmpt, rstd_ap, keys_in, kname):
        for kc in range(nk):
            s = sq[kc % 2]
            act(s.ap(), src_fn(kc), AF.Square, keys_in, [s.name])
            mm(stat_ps.ap(), onesr.ap(), s.ap(), kc == 0, kc == nk - 1, [s.name, "onesr"], [stat_ps.name])
        sqrt_eps(tmpt.ap(), stat_ps.ap(), dim, [stat_ps.name], [tmpt.name])
        recip(rstd_ap, tmpt.ap(), [tmpt.name], [kname])

    def load_cast_slab(src, stage, dst, k):
        dma(stage.ap(), src, [], [stage.name], "w_" + stage.name)
        cp("pool", dst.ap(), stage.ap(), [stage.name], [dst.name])

    def ffn_phase(l, which):
        nonlocal st_ph
        pre_i, post_i = (0, 1) if which == 0 else (4, 5)
        with ExitStack() as st_ph:
            hx = T("hx", [128, KC, 512])
            xn = T("xn", [128, KC, 1024], F32R)
            actT = T("actT", [128, FC, 1024], F32R)
            stg = [T(f"stg{i}", [128, 1024]) for i in range(3)]
            wgc = [T(f"wgc{i}", [128, 1024], F32R) for i in range(2)]
            wuc = [T(f"wuc{i}", [128, 1024], F32R) for i in range(2)]
            wdc = [T(f"wdc{i}", [128, 1024], F32R) for i in range(2)]
            fsb = T("fsb", [128, KC, 512])
            sq = [T(f"sq{i}", [128, 512], F32R) for i in range(2)]
            rstd = T("rstd", [128, 1024])
            tm = [T(f"tm{i}", [128, 512]) for i in range(2)]
            hxr = xn.ap()
            nst = 0
            for gi in range(4):
                t0 = gi * 1024
                for hf in range(2):
                    sl = slice(hf * 512, hf * 512 + 512)
                    for kc in range(KC):
                        dma(hx.ap()[:, kc, :], hTv[:, kc, t0 + hf * 512:t0 + hf * 512 + 512], ["hT"], [f"hx{kc}"], f"hx{kc}")
                    for kc in range(KC):
                        s = sq[kc % 2]
                        act(s.ap(), hx.ap()[:, kc, :], AF.Square, [f"hx{kc}"], [s.name])
                        mm(ps[7].ap(), onesr.ap(), s.ap(), kc == 0, kc == KC - 1, [s.name, "onesr"], ["ps7"])
                    sqrt_eps(tm[0].ap(), ps[7].ap(), D, ["ps7"], ["tm0"])
                    recip(rstd.ap()[:, sl], tm[0].ap(), ["tm0"], [f"rstd{hf}"])
                    for kc in range(KC):
                        stt(hxr[:, kc, sl], hx.ap()[:, kc, :], gcol(l, pre_i, kc), rstd.ap()[:, sl],
                            ALU.mult, ALU.mult, [f"hx{kc}", f"rstd{hf}", "gn"], [f"xn{kc}_{hf}"])
                for fc in range(FC):
                    sg_, su_ = stg[nst % 3], stg[(nst + 1) % 3]
                    nst += 2
                    load_cast_slab(wg[which][l, fc], sg_, wgc[fc % 2], "g")
                    load_cast_slab(wu[which][l, fc], su_, wuc[fc % 2], "u")
                    for hf in range(2):
                        sl = slice(hf * 512, hf * 512 + 512)
                        pg, pu = ps[2 * hf], ps[2 * hf + 1]
                        for kc in range(KC):
                            mm(pg.ap(), wgc[fc % 2].ap()[:, kc * 128:(kc + 1) * 128], hxr[:, kc, sl],
                               kc == 0, kc == KC - 1, [wgc[fc % 2].name, f"xn{kc}_{hf}"], [pg.name])
                        for kc in range(KC):
                            mm(pu.ap(), wuc[fc % 2].ap()[:, kc * 128:(kc + 1) * 128], hxr[:, kc, sl],
                               kc == 0, kc == KC - 1, [wuc[fc % 2].name, f"xn{kc}_{hf}"], [pu.name])
                        act(tm[hf].ap(), pg.ap(), AF.Silu, [pg.name], [tm[hf].name])
                        tt("dve", actT.ap()[:, fc, sl], tm[hf].ap(), pu.ap(), ALU.mult,
                           [tm[hf].name, pu.name], [f"act{fc}_{hf}"])
                for hf in range(2):
                    sl = slice(hf * 512, hf * 512 + 512)
                    for fc in range(FC):
                        sd_ = stg[nst % 3]
                        nst += 1
                        load_cast_slab(wd[which][l, fc], sd_, wdc[fc % 2], "d")
                        for oc in range(KC):
                            mm(ps[oc].ap(), wdc[fc % 2].ap()[:, oc * 128:(oc + 1) * 128], actT.ap()[:, fc, sl],
                               fc == 0, fc == FC - 1, [wdc[fc % 2].name, f"act{fc}_{hf}"], [ps[oc].name])
                    for oc in range(KC):
                        cp("act", fsb.ap()[:, oc, :], ps[oc].ap(), [ps[oc].name], [f"fsb{oc}"])
                    for oc in range(KC):
                        s = sq[oc % 2]
                        act(s.ap(), fsb.ap()[:, oc, :], AF.Square, [f"fsb{oc}"], [s.name])
                        mm(ps[0].ap(), onesr.ap(), s.ap(), oc == 0, oc == KC - 1, [s.name, "onesr"], ["ps0"])
                    sqrt_eps(tm[0].ap(), ps[0].ap(), D, ["ps0"], ["tm0"])
                    recip(tm[1].ap(), tm[0].ap(), ["tm0"], ["tm1"])
                    for oc in range(KC):
                        dma(hx.ap()[:, oc, :], hTv[:, oc, t0 + hf * 512:t0 + hf * 512 + 512], ["hT"],
                            [f"hx{oc}"], f"hx{oc}")
                        stt(fsb.ap()[:, oc, :], fsb.ap()[:, oc, :], gcol(l, post_i, oc, half=True), tm[1].ap(),
                            ALU.mult, ALU.mult, [f"fsb{oc}", "tm1", "gnh"], [f"fsb{oc}"])
                        tt("dve", hx.ap()[:, oc, :], hx.ap()[:, oc, :], fsb.ap()[:, oc, :], ALU.add,
                           [f"hx{oc}", f"fsb{oc}"], [f"hx{oc}"])
                        dma(hTv[:, oc, t0 + hf * 512:t0 + hf * 512 + 512], hx.ap()[:, oc, :], [f"hx{oc}"],
                            ["hT"], "hst")
            P.barrier()
            P.emit()

    def inproj_phase(l):
        nonlocal st_ph
        with ExitStack() as st_ph:
            hx = T("hx", [128, KC, 1024])
            xn = T("xn", [128, KC, 1024], F32R)
            stg = [T(f"stg{i}", [128, 1024]) for i in range(2)]
            wc = [T(f"wc{i}", [128, 1024], F32R) for i in range(2)]
            ost = [T(f"ost{i}", [128, 1024]) for i in range(2)]
            sq = [T(f"sq{i}", [128, 512], F32R) for i in range(2)]
            rstd = T("rstd", [128, 1024])
            tm0 = T("tm0", [128, 512])
            hxr = xn.ap()
            n = 0
            for gi in range(4):
                t0 = gi * 1024
                for kc in range(KC):
                    dma(hx.ap()[:, kc, :], hTv[:, kc, t0:t0 + 1024], ["hT"], [f"hx{kc}"], f"hx{kc}")
                for hf in range(2):
                    sl = slice(hf * 512, hf * 512 + 512)
                    rms_rstd(lambda kc: hx.ap()[:, kc, sl], KC, D, sq, ps[7], tm0, rstd.ap()[:, sl],
                             [f"hx{kc}" for kc in range(KC)], f"rstd{hf}")
                    for kc in range(KC):
                        stt(hxr[:, kc, sl], hx.ap()[:, kc, sl], gcol(l, 2, kc), rstd.ap()[:, sl],
                            ALU.mult, ALU.mult, [f"hx{kc}", f"rstd{hf}", "gn"], [f"xn{kc}_{hf}"])
                for fc in range(16):
                    load_cast_slab(win[l, fc], stg[n % 2], wc[n % 2], "w")
                    o = ost[n % 2]
                    for hf in range(2):
                        sl = slice(hf * 512, hf * 512 + 512)
                        pp = ps[(2 * n + hf) % 4]
                        for kc in range(KC):
                            mm(pp.ap(), wc[n % 2].ap()[:, kc * 128:(kc + 1) * 128], hxr[:, kc, sl],
                               kc == 0, kc == KC - 1, [wc[n % 2].name, f"xn{kc}_{hf}"], [pp.name])
                        if fc < 12:
                            cp("act" if hf == 0 else "dve", o.ap()[:, sl], pp.ap(), [pp.name], [o.name + str(hf)])
                        else:
                            oap = fap(o, [(1, 64), (128, 8)], off=hf * 64)
                            cp("act" if hf == 0 else "dve", oap, pp.ap().rearrange("p (c j) -> p c j", j=8),
                               [pp.name], [o.name + str(hf)])
                    if fc < 12:
                        dma(qkvT[fc * 128:(fc + 1) * 128, t0:t0 + 1024], o.ap(), [o.name + "0", o.name + "1"],
                            ["qkvT"], "qst")
                    else:
                        r0 = (fc - 12) * 128
                        dma(uP[r0:r0 + 128, :, gi * 128:(gi + 1) * 128], o.ap().rearrange("p (j c) -> p j c", j=8),
                            [o.name + "0", o.name + "1"], ["uP"], "qst")
                    n += 1
            P.barrier()
            P.emit()

    def rope_phase():
        nonlocal st_ph
        with ExitStack() as st_ph:
            tc_ = T("tc", [128, S])
            tsn = T("tsn", [128, S])
            xr = T("xr", [128, S])
            xp = T("xp", [128, S])
            dma(tc_.ap(), ropeC, ["ropeC"], ["tc"], "rp0")
            dma(tsn.ap(), ropeS, ["ropeS"], ["tsn"], "rp1")
            for base in (0, 512):
                for h in range(8):
                    r = base + h * 64
                    dma(xr.ap()[h * 16:h * 16 + 16, :], qkvT[r:r + 16, :], ["qkvT"], ["xr"], "rp2")
                    dma(xp.ap()[h * 16:h * 16 + 8, :], qkvT[r + 8:r + 16, :], ["qkvT"], ["xp"], "rp3")
                    dma(xp.ap()[h * 16 + 8:h * 16 + 16, :], qkvT[r:r + 8, :], ["qkvT"], ["xp"], "rp3")
                tt("dve", xr.ap(), xr.ap(), tc_.ap(), ALU.mult, ["xr", "tc"], ["xr"])
                tt("pool", xp.ap(), xp.ap(), tsn.ap(), ALU.mult, ["xp", "tsn"], ["xp"])
                tt("dve", xr.ap(), xr.ap(), xp.ap(), ALU.add, ["xr", "xp"], ["xr"])
                for h in range(8):
                    r = base + h * 64
                    dma(qkvT[r:r + 16, :], xr.ap()[h * 16:h * 16 + 16, :], ["xr"], ["qkvT"], "rp4")
            P.barrier()
            P.emit()

    def attn_phase():
        nonlocal st_ph
        with ExitStack() as st_ph:
            stf = [T(f"stf{i}", [64, S]) for i in range(3)]
            qb = T("qb", [64, S], BF16)
            kb = T("kb", [64, S], BF16)
            vb = T("vb", [64, S], BF16)
            acc = T("acc", [128, S])
            mkf = T("mkf", [128, 256])
            mk = T("mk", [128, 256], BF16)
            shf = T("shf", [128, 64])
            identb = T("identb", [128, 128], BF16)
            E = [T(f"E{i}", [128, 256], BF16) for i in range(2)]
            VT = [T(f"VT{i}", [128, 128], BF16) for i in range(3)]
            rec = T("rec", [64, 512])
            oh = T("oh", [64, S])
            psVb = [ps[4].ap().bitcast(BF16), ps[5].ap().bitcast(BF16)]
            dma(mkf.ap(), c_mask, [], ["mkf"], "a0")
            dma(shf.ap(), c_shift, [], ["shf"], "a1")
            cp("dve", mk.ap(), mkf.ap(), ["mkf"], ["mk"])
            cp("dve", identb.ap(), ident.ap(), ["ident"], ["identb"])
            for i in range(3):
                P.op("dve", (lambda t: (lambda e: e.memset(t.ap()[:, 64:128], 1.0)))(VT[i]), [], [VT[i].name])
            un = 0
            for h in range(8):
                for i, (dst, base) in enumerate(((qb, 0), (kb, 512), (vb, 1024))):
                    dma(stf[i].ap(), qkvT[base + h * 64:base + h * 64 + 64, :], ["qkvT"], [stf[i].name], f"a2{i}")
                    cp("act" if i == 0 else "pool", dst.ap(), stf[i].ap(), [stf[i].name], [dst.name])
                first = True
                for r in (1, 4, 16):
                    nb = S // r // 128
                    for j in range(r):
                        def tok(b):
                            s0 = j + r * 128 * b
                            return slice(s0, s0 + r * 127 + 1, r)
                        for b in range(nb):
                            vt = VT[un % 3]
                            vprev = VT[(un - 1) % 3]
                            pA = ps[un % 2]
                            pV = psVb[un % 2]
                            pO = ps[6 + un % 2]
                            e_ = E[un % 2]
                            ncol = 256 if b > 0 else 128
                            transp(pV[:, 0:64], vb.ap()[:, tok(b)], identb.ap()[0:64, 0:64], ["vb", "identb"], [f"pV{un % 2}"])
                            cp("act", vt.ap()[:, 0:64], pV[:, 0:64], [f"pV{un % 2}"], [vt.name])
                            mm(pA.ap()[:, 0:128], kb.ap()[:, tok(b)], qb.ap()[:, tok(b)], True, True,
                               ["kb", "qb"], [pA.name], final=(b == 0))
                            if b > 0:
                                mm(pA.ap()[:, 128:256], kb.ap()[:, tok(b - 1)], qb.ap()[:, tok(b)], True, True,
                                   ["kb", "qb"], [pA.name], final=True)
                            act(e_.ap()[:, 0:ncol], pA.ap()[:, 0:ncol], AF.Exp, [pA.name], [e_.name], scale=0.125)
                            tt("dve", e_.ap()[:, 0:ncol], e_.ap()[:, 0:ncol], mk.ap()[:, 0:ncol], ALU.mult,
                               [e_.name, "mk"], [e_.name])
                            mm(pO.ap()[:, 0:128], vt.ap(), e_.ap()[:, 0:128], True, b == 0, [vt.name, e_.name], [pO.name])
                            if b > 0:
                                mm(pO.ap()[:, 0:128], vprev.ap(), e_.ap()[:, 128:256], False, True,
                                   [vprev.name, e_.name], [pO.name])
                            if first:
                                cp("dve", acc.ap()[:, tok(b)], pO.ap()[:, 0:128], [pO.name], ["acc"])
                            else:
                                tt("dve", acc.ap()[:, tok(b)], acc.ap()[:, tok(b)], pO.ap()[:, 0:128], ALU.add,
                                   [pO.name, "acc"], ["acc"])
                            un += 1
                    first = False
                for tg in range(8):
                    sl = slice(tg * 512, tg * 512 + 512)
                    pS = ps[2 + tg % 2]
                    mm(pS.ap()[0:64, :], shf.ap(), acc.ap()[:, sl], True, True, ["shf", "acc"], [pS.name])
                    recip(rec.ap(), pS.ap()[0:64, :], [pS.name], ["rec"])
                    tt("dve", oh.ap()[:, sl], acc.ap()[0:64, sl], rec.ap(), ALU.mult, ["acc", "rec"], ["oh"])
                dma(attnT[h * 64:h * 64 + 64, :], oh.ap(), ["oh"], ["attnT"], "a3")
            P.barrier()
            P.emit()

    def s5_phase(l):
        nonlocal st_ph
        with ExitStack() as st_ph:
            LR = T("LR", [128, 32]); LI = T("LI", [128, 32]); LD = T("LD", [128, 32])
            BX1 = T("BX1", [128, 512]); BX2 = T("BX2", [128, 512])
            CY1 = T("CY1", [128, 512]); CY2 = T("CY2", [128, 512])
            DST = T("DST", [128, 32])
            expo = T("expo", [128, 24])
            caus = T("caus", [128, 128])
            dtt = T("dtt", [128, 32]); aa = T("aa", [128, 32]); th = T("th", [128, 32])
            yS = T("yS", [128, 768]); yC = T("yC", [128, 768]); yI = T("yI", [128, 768], I32)
            yT_ = T("yT_", [128, 768]); PM = T("PM", [128, 768])
            PWR = T("PWR", [128, 768]); PWI = T("PWI", [128, 768])
            t32 = [T(f"t32_{i}", [128, 32]) for i in range(4)]
            KR = T("KR", [128, 32]); KI = T("KI", [128, 32])
            Bs1 = T("Bs1", [128, 512]); Bs2 = T("Bs2", [128, 512]); tb = T("tb", [128, 512])
            KmE = T("KmE", [128, 4096]); QmN = T("QmN", [128, 4096]); QmP = T("QmP", [128, 4096])
            tbig = T("tbig", [128, 4096])
            A1 = T("A1", [128, 9 * 32]); A2 = T("A2", [128, 9 * 32]); A2s = T("A2s", [128, 9 * 32])
            MI = T("MI", [128, 9 * 32])
            ust = [T(f"ust{i}", [128, 512]) for i in range(2)]
            T0 = [T(f"T0_{i}", [128, 128]) for i in range(2)]
            KmT = [T(f"KmT{i}", [128, 128]) for i in range(2)]
            KmTs = [T(f"KmTs{i}", [128, 128]) for i in range(2)]
            Z = [T(f"Z{i}", [128, 512]) for i in range(2)]
            Zs = [T(f"Zs{i}", [128, 512]) for i in range(2)]
            y1 = [T(f"sy1_{i}", [128, 512]) for i in range(2)]
            y2 = [T(f"sy2_{i}", [128, 512]) for i in range(2)]
            y3 = [T(f"sy3_{i}", [128, 512]) for i in range(2)]
            for t, src, k in ((LR, s_lr, "LR"), (LI, s_li, "LI"), (LD, s_ld, "LD"), (BX1, s_bx1, "BX1"),
                              (BX2, s_bx2, "BX2"), (CY1, s_cy1, "CY1"), (CY2, s_cy2, "CY2"), (DST, s_dst, "DST")):
                dma(t.ap(), src[l], [], [k], "s_" + k)
            dma(expo.ap(), c_expo, [], ["expo"], "s_e")
            dma(caus.ap(), c_causal, [], ["caus"], "s_c")
            sgnA = cvec.ap()[:, 0:1]; sgnB = cvec.ap()[:, 1:2]
            ts("dve", BX2.ap(), BX2.ap(), sgnA, None, ALU.mult, ALU.bypass, ["BX2"], ["BX2"])
            ts("dve", CY1.ap(), CY1.ap(), sgnB, None, ALU.mult, ALU.bypass, ["CY1"], ["CY1"])
            ts("dve", CY2.ap(), CY2.ap(), -1.0, None, ALU.mult, ALU.bypass, ["CY2"], ["CY2"])
            act(dtt.ap(), LD.ap(), AF.Exp, ["LD"], ["dtt"])
            tt("dve", aa.ap(), LR.ap(), dtt.ap(), ALU.mult, ["LR", "dtt"], ["aa"])
            tt("dve", th.ap(), LI.ap(), dtt.ap(), ALU.mult, ["LI", "dtt"], ["th"])
            ts("dve", th.ap(), th.ap(), 1.0 / TWO_PI, None, ALU.mult, ALU.bypass, ["th"], ["th"])
            g3 = lambda t: fap(t, [(1, 32), (0, 24)])
            e3 = fap(expo, [(0, 32), (1, 24)])
            v3 = lambda t: fap(t, [(24, 32), (1, 24)])
            tt("dve", v3(yS), g3(th), e3, ALU.mult, ["th", "expo"], ["yS"])
            ts("dve", yC.ap(), yS.ap(), 0.25, None, ALU.add, ALU.bypass, ["yS"], ["yC"])
            tt("dve", v3(PM), g3(aa), e3, ALU.mult, ["aa", "expo"], ["PM"])
            act(PM.ap(), PM.ap(), AF.Exp, ["PM"], ["PM"])
            range_reduce("dve", yS.ap(), yI.ap(), yT_.ap(), "yS")
            range_reduce("dve", yC.ap(), yI.ap(), yT_.ap(), "yC")
            act(yS.ap(), yS.ap(), AF.Sin, ["yS"], ["yS"], scale=SIN_SCALE)
            act(yC.ap(), yC.ap(), AF.Sin, ["yC"], ["yC"], scale=SIN_SCALE)
            tt("dve", PWR.ap(), PM.ap(), yC.ap(), ALU.mult, ["PM", "yC"], ["PWR"])
            tt("dve", PWI.ap(), PM.ap(), yS.ap(), ALU.mult, ["PM", "yS"], ["PWI"])
            pw = lambda t, e: fap(t, [(24, 32)], off=e)
            ER, EI = pw(PWR, 8), pw(PWI, 8)
            a0, a1, a2, a3 = [t.ap() for t in t32]
            ts("dve", a0, ER, -1.0, None, ALU.add, ALU.bypass, ["PWR"], ["a0"])
            tt("dve", a1, LR.ap(), LR.ap(), ALU.mult, ["LR"], ["a1"])
            tt("dve", a2, LI.ap(), LI.ap(), ALU.mult, ["LI"], ["a2"])
            tt("dve", a1, a1, a2, ALU.add, ["a1", "a2"], ["a1"])
            recip(a1, a1, ["a1"], ["a1"])
            tt("dve", a2, a0, LR.ap(), ALU.mult, ["a0", "LR"], ["a2"])
            tt("dve", a3, EI, LI.ap(), ALU.mult, ["PWI", "LI"], ["a3"])
            tt("dve", a2, a2, a3, ALU.add, ["a2", "a3"], ["a2"])
            tt("dve", KR.ap(), a2, a1, ALU.mult, ["a2", "a1"], ["KR"])
            tt("dve", a2, EI, LR.ap(), ALU.mult, ["PWI", "LR"], ["a2"])
            tt("dve", a3, a0, LI.ap(), ALU.mult, ["a0", "LI"], ["a3"])
            tt("dve", a2, a2, a3, ALU.subtract, ["a2", "a3"], ["a2"])
            tt("dve", KI.ap(), a2, a1, ALU.mult, ["a2", "a1"], ["KI"])
            kb3 = lambda t: fap(t, [(1, 32), (0, 16)])
            b3 = lambda t: fap(t, [(16, 32), (1, 16)])
            tt("dve", b3(Bs1), kb3(KR), b3(BX1), ALU.mult, ["KR", "BX1"], ["Bs1"])
            tt("dve", b3(tb), kb3(KI), b3(BX2), ALU.mult, ["KI", "BX2"], ["tb"])
            tt("dve", Bs1.ap(), Bs1.ap(), tb.ap(), ALU.add, ["Bs1", "tb"], ["Bs1"])
            tt("dve", b3(Bs2), kb3(KR), b3(BX2), ALU.mult, ["KR", "BX2"], ["Bs2"])
            tt("dve", b3(tb), kb3(KI), b3(BX1), ALU.mult, ["KI", "BX1"], ["tb"])
            tt("dve", Bs2.ap(), Bs2.ap(), tb.ap(), ALU.subtract, ["Bs2", "tb"], ["Bs2"])
            def big(dst, e0, X1, X2, k):
                for g0 in range(0, 32, 8):
                    for g in range(g0, g0 + 8):
                        pr = fap(PWR, [(1, 8), (0, 16)], off=g * 24 + e0)
                        pi = fap(PWI, [(1, 8), (0, 16)], off=g * 24 + e0)
                        x1 = fap(X1, [(0, 8), (1, 16)], off=g * 16)
                        x2 = fap(X2, [(0, 8), (1, 16)], off=g * 16)
                        o = fap(dst, [(16, 8), (1, 16)], off=g * 128)
                        tq = fap(tbig, [(16, 8), (1, 16)], off=g * 128)
                        eng = "dve" if g % 2 == 0 else "pool"
                        tt(eng, o, pr, x1, ALU.mult, ["PWR", X1.name], [f"{k}{g}"])
                        tt(eng, tq, pi, x2, ALU.mult, ["PWI", X2.name], [f"tq{g}"])
                        tt(eng, o, o, tq, ALU.add, [f"{k}{g}", f"tq{g}"], [f"{k}{g}"])
            big(KmE, 0, Bs1, Bs2, "KmE")
            big(QmN, 8, CY1, CY2, "QmN")
            big(QmP, 16, CY1, CY2, "QmP")
            cp("dve", A1.ap()[:, 0:32], pw(PWR, 15), ["PWR"], ["A1"])
            cp("dve", MI.ap()[:, 0:32], pw(PWI, 15), ["PWI"], ["MI"])
            for s_ in range(8):
                r0 = A1.ap()[:, s_ * 32:(s_ + 1) * 32]; i0 = MI.ap()[:, s_ * 32:(s_ + 1) * 32]
                r1 = A1.ap()[:, (s_ + 1) * 32:(s_ + 2) * 32]; i1 = MI.ap()[:, (s_ + 1) * 32:(s_ + 2) * 32]
                tt("dve", a0, r0, r0, ALU.mult, ["A1"], ["a0"])
                tt("dve", a2, i0, i0, ALU.mult, ["MI"], ["a2"])
                tt("dve", r1, a0, a2, ALU.subtract, ["a0", "a2"], ["A1"])
                tt("dve", a3, r0, i0, ALU.mult, ["A1", "MI"], ["a3"])
                ts("dve", i1, a3, 2.0, None, ALU.mult, ALU.bypass, ["a3"], ["MI"])
            ts("dve", A2.ap(), MI.ap(), sgnA, None, ALU.mult, ALU.bypass, ["MI"], ["A2"])
            ts("dve", A2s.ap(), MI.ap(), sgnB, None, ALU.mult, ALU.bypass, ["MI"], ["A2s"])
            for g in range(32):
                b_ = g % 2
                u = ust[b_]
                rr = 16 * g
                for j in range(8):
                    dma(u.ap()[16 * j:16 * j + 16, :], uP[rr:rr + 16, j, :], ["uP"], [u.name], f"su{b_}")
                kme = KmE.ap()[:, g * 128:(g + 1) * 128]
                qmn = QmN.ap()[:, g * 128:(g + 1) * 128]
                qmp = QmP.ap()[:, g * 128:(g + 1) * 128]
                pT0 = ps[b_ * 4 + 0]; pK = ps[b_ * 4 + 1]; pZ = ps[b_ * 4 + 2]; pY = ps[b_ * 4 + 3]
                mm(pT0.ap()[:, 0:128], kme, qmp, True, True, [f"KmE{g}", f"QmP{g}"], [pT0.name])
                tt("dve", T0[b_].ap(), pT0.ap()[:, 0:128], caus.ap(), ALU.mult, [pT0.name, "caus"], [T0[b_].name])
                stt(T0[b_].ap(), ident.ap(), DST.ap()[:, g:g + 1], T0[b_].ap(), ALU.mult, ALU.add,
                    ["ident", "DST", T0[b_].name], [T0[b_].name])
                transp(pK.ap()[:, 0:128], kme, ident.ap(), [f"KmE{g}", "ident"], [pK.name])
                cp("act", KmT[b_].ap(), pK.ap()[:, 0:128], [pK.name], [KmT[b_].name])
                cp("act", KmTs[b_].ap()[:, 0:64], pK.ap()[:, 64:128], [pK.name], [KmTs[b_].name])
                cp("act", KmTs[b_].ap()[:, 64:128], pK.ap()[:, 0:64], [pK.name], [KmTs[b_].name])
                mm(pZ.ap(), KmT[b_].ap(), u.ap(), True, True, [KmT[b_].name, u.name], [pZ.name])
                cp("act", Z[b_].ap(), pZ.ap(), [pZ.name], [Z[b_].name])
                mm(pZ.ap(), KmTs[b_].ap(), u.ap(), True, True, [KmTs[b_].name, u.name], [pZ.name])
                cp("act", Zs[b_].ap(), pZ.ap(), [pZ.name], [Zs[b_].name])
                mm(pY.ap(), T0[b_].ap(), u.ap(), True, False, [T0[b_].name, u.name], [pY.name], final=True)
                zk, zsk = Z[b_].name, Zs[b_].name

                def level(s_, dst, src):
                    c1 = A1.ap()[:, s_ * 32 + g:s_ * 32 + g + 1]
                    c2 = A2.ap()[:, s_ * 32 + g:s_ * 32 + g + 1]
                    c2s = A2s.ap()[:, s_ * 32 + g:s_ * 32 + g + 1]
                    zd, zsd = Z[b_].ap()[:, dst], Zs[b_].ap()[:, dst]
                    zr, zsr = Z[b_].ap()[:, src], Zs[b_].ap()[:, src]
                    stt(zd, zr, c1, zd, ALU.mult, ALU.add, [zk, "A1"], [zk])
                    stt(zd, zsr, c2, zd, ALU.mult, ALU.add, [zk, zsk, "A2"], [zk])
                    stt(zsd, zsr, c1, zsd, ALU.mult, ALU.add, [zsk, "A1"], [zsk])
                    stt(zsd, zr, c2s, zsd, ALU.mult, ALU.add, [zk, zsk, "A2s"], [zsk])
                for s_ in range(9):
                    st2 = 2 << s_
                    level(s_, slice(st2 - 1, 512, st2), slice((1 << s_) - 1, 512 - (1 << s_), st2))
                for s_ in range(7, -1, -1):
                    st2 = 2 << s_
                    level(s_, slice(st2 + (1 << s_) - 1, 512, st2), slice(st2 - 1, 512 - (1 << s_), st2))
                mm(pY.ap()[:, 1:512], qmn, Z[b_].ap()[:, 0:511], False, True, [f"QmN{g}", zk], [pY.name])
                cp("act", y1[b_].ap(), pY.ap(), [pY.name], [y1[b_].name])
                tt("pool", y2[b_].ap(), y1[b_].ap(), y1[b_].ap(), ALU.mult, [y1[b_].name], [y2[b_].name])
                ts("pool", y2[b_].ap(), y2[b_].ap(), 0.044715, 1.0, ALU.mult, ALU.add, [y2[b_].name], [y2[b_].name])
                tt("pool", y2[b_].ap(), y2[b_].ap(), y1[b_].ap(), ALU.mult, [y1[b_].name, y2[b_].name], [y2[b_].name])
                act(y2[b_].ap(), y2[b_].ap(), AF.Sigmoid, [y2[b_].name], [y2[b_].name], scale=1.5957691216057308)
                tt("pool", y3[b_].ap(), y2[b_].ap(), y1[b_].ap(), ALU.mult, [y1[b_].name, y2[b_].name], [y3[b_].name])
                for i in range(8):
                    dma(ysc[rr:rr + 16, i, :], y3[b_].ap()[16 * i:16 * i + 16, :], [y3[b_].name], ["ysc"], "sy")
            P.barrier()
            P.emit()

    def mixout_phase(l):
        nonlocal st_ph
        with ExitStack() as st_ph:
            ssmn = T("ssmn", [128, 4, S], F32R)
            woc = T("woc", [128, 8, D], F32R)
            wgl = T("wgl", [128, 4, 512], F32R)
            stg = [T(f"stg{i}", [128, 1024]) for i in range(2)]
            yP = T("yP", [128, 4, 512]); yPr = T("yPr", [128, 4, 512], F32R)
            so = T("so", [128, 4, 512])
            at = T("at", [128, 4, 512]); atn = T("atn", [128, 4, 512], F32R)
            fsb = T("fsb", [128, KC, 512]); hx = T("hx", [128, KC, 512])
            sq = [T(f"sq{i}", [128, 512], F32R) for i in range(2)]
            tm0 = T("tm0", [128, 512]); tm1 = T("tm1", [128, 512]); sg = T("sg", [128, 512])
            for oc in range(8):
                dma(stg[oc % 2].ap(), wout[l, oc], [], [stg[oc % 2].name], "w_" + stg[oc % 2].name)
                cp("pool", woc.ap()[:, oc, :], stg[oc % 2].ap(), [stg[oc % 2].name], [f"woc{oc}"])
            for oc in range(4):
                dma(stg[oc % 2].ap()[:, 0:512], wglu[l, oc], [], [stg[oc % 2].name], "w_" + stg[oc % 2].name)
                cp("pool", wgl.ap()[:, oc, :], stg[oc % 2].ap()[:, 0:512], [stg[oc % 2].name], [f"wgl{oc}"])
            yscv = ysc_src.rearrange("(kc p) i c -> p kc i c", p=128)
            for i in range(8):
                for kc in range(4):
                    dma(yP.ap()[:, kc, :], yscv[:, kc, i, :], ["ysc"], [f"yP{kc}"], f"m_yP{kc}")
                    cp("pool", yPr.ap()[:, kc, :], yP.ap()[:, kc, :], [f"yP{kc}"], [f"yPr{kc}"])
                for oc in range(4):
                    for kc in range(4):
                        mm(ps[oc].ap(), wgl.ap()[:, oc, kc * 128:(kc + 1) * 128], yPr.ap()[:, kc, :], kc == 0, kc == 3,
                           [f"wgl{oc}", f"yPr{kc}"], [ps[oc].name])
                    act(sg.ap(), ps[oc].ap(), AF.Sigmoid, [ps[oc].name, "bgl"], ["sg"],
                        bias=bgl.ap()[:, l * 4 + oc:l * 4 + oc + 1])
                    tt("dve", so.ap()[:, oc, :], yP.ap()[:, oc, :], sg.ap(), ALU.mult, [f"yP{oc}", "sg"], [f"so{oc}"])
                rms_rstd(lambda kc: so.ap()[:, kc, :], 4, 512, sq, ps[7], tm0, tm1.ap(),
                         [f"so{kc}" for kc in range(4)], "tm1")
                for oc in range(4):
                    oap = fap(ssmn, [(8, 512)], off=oc * S + i)
                    stt(oap, so.ap()[:, oc, :], gn2.ap()[:, (l * 2 + 1) * 4 + oc:(l * 2 + 1) * 4 + oc + 1], tm1.ap(),
                        ALU.mult, ALU.mult, [f"so{oc}", "tm1", "gn2"], [f"ssmn{oc}"])
            atv = attnT_src.rearrange("(kc p) t -> p kc t", p=128)
            for tg in range(8):
                sl = slice(tg * 512, tg * 512 + 512)
                for kc in range(4):
                    dma(at.ap()[:, kc, :], atv[:, kc, sl], ["attnT"], [f"at{kc}"], f"m_at{kc}")
                rms_rstd(lambda kc: at.ap()[:, kc, :], 4, 512, sq, ps[7], tm0, tm1.ap(),
                         [f"at{kc}" for kc in range(4)], "tm1")
                for kc in range(4):
                    stt(atn.ap()[:, kc, :], at.ap()[:, kc, :], gn2.ap()[:, (l * 2) * 4 + kc:(l * 2) * 4 + kc + 1],
                        tm1.ap(), ALU.mult, ALU.mult, [f"at{kc}", "tm1", "gn2"], [f"atn{kc}"])
                for oc in range(KC):
                    pp = ps[oc % 4]
                    for kc in range(KC):
                        rhs = atn.ap()[:, kc, :] if kc < 4 else ssmn.ap()[:, kc - 4, sl]
                        rk = f"atn{kc}" if kc < 4 else f"ssmn{kc - 4}"
                        mm(pp.ap(), woc.ap()[:, oc, kc * 128:(kc + 1) * 128], rhs, kc == 0, kc == KC - 1,
                           [f"woc{oc}", rk], [pp.name])
                    cp("act", fsb.ap()[:, oc, :], pp.ap(), [pp.name], [f"fsb{oc}"])
                rms_rstd(lambda kc: fsb.ap()[:, kc, :], KC, D, sq, ps[6], tm0, tm1.ap(),
                         [f"fsb{kc}" for kc in range(KC)], "tm1")
                for oc in range(KC):
                    dma(hx.ap()[:, oc, :], hTv[:, oc, sl], ["hT"], [f"hx{oc}"], f"hx{oc}")
                    stt(fsb.ap()[:, oc, :], fsb.ap()[:, oc, :], gcol(l, 3, oc), tm1.ap(), ALU.mult, ALU.mult,
                        [f"fsb{oc}", "tm1", "gn"], [f"fsb{oc}"])
                    tt("dve", hx.ap()[:, oc, :], hx.ap()[:, oc, :], fsb.ap()[:, oc, :], ALU.add,
                       [f"hx{oc}", f"fsb{oc}"], [f"hx{oc}"])
                    dma(hTv[:, oc, sl], hx.ap()[:, oc, :], [f"hx{oc}"], ["hT"], "hst")
            P.barrier()
            P.emit()

    def ple_phase(l):
        nonlocal st_ph
        with ExitStack() as st_ph:
            wgc = T("wgc", [128, 8, D], F32R)
            wuc = T("wuc", [128, 8, 256], F32R)
            stg = [T(f"stg{i}", [128, 1024]) for i in range(2)]
            hx = T("hx", [128, KC, 512]); hr = T("hr", [128, KC, 512], F32R)
            pp_ = T("pp_", [128, 2, 512]); ppr = T("ppr", [128, 2, 512], F32R)
            sgm = T("sgm", [128, KC, 512]); fsb = T("fsb", [128, KC, 512])
            sq = [T(f"sq{i}", [128, 512], F32R) for i in range(2)]
            tm0 = T("tm0", [128, 512]); tm1 = T("tm1", [128, 512])
            for oc in range(8):
                dma(stg[oc % 2].ap(), plewg[l, oc], [], [stg[oc % 2].name], "w_" + stg[oc % 2].name)
                cp("pool", wgc.ap()[:, oc, :], stg[oc % 2].ap(), [stg[oc % 2].name], [f"wgc{oc}"])
            for oc in range(8):
                dma(stg[oc % 2].ap()[:, 0:256], plewu[l, oc], [], [stg[oc % 2].name], "w_" + stg[oc % 2].name)
                cp("pool", wuc.ap()[:, oc, :], stg[oc % 2].ap()[:, 0:256], [stg[oc % 2].name], [f"wuc{oc}"])
            pv = pT[l].rearrange("(kc p) t -> p kc t", p=128)
            for tg in range(8):
                sl = slice(tg * 512, tg * 512 + 512)
                for kc in range(KC):
                    dma(hx.ap()[:, kc, :], hTv[:, kc, sl], ["hT"], [f"hx{kc}"], f"hx{kc}")
                    cp("pool" if kc % 2 else "act", hr.ap()[:, kc, :], hx.ap()[:, kc, :], [f"hx{kc}"], [f"hr{kc}"])
                for kc in range(2):
                    dma(pp_.ap()[:, kc, :], pv[:, kc, sl], [], [f"pp{kc}"], f"p_pp{kc}")
                    cp("pool", ppr.ap()[:, kc, :], pp_.ap()[:, kc, :], [f"pp{kc}"], [f"ppr{kc}"])
                for oc in range(KC):
                    pg = ps[oc % 4]
                    for kc in range(KC):
                        mm(pg.ap(), wgc.ap()[:, oc, kc * 128:(kc + 1) * 128], hr.ap()[:, kc, :], kc == 0, kc == KC - 1,
                           [f"wgc{oc}", f"hr{kc}"], [pg.name])
                    act(sgm.ap()[:, oc, :], pg.ap(), AF.Sigmoid, [pg.name], [f"sgm{oc}"])
                    pu = ps[4 + oc % 2]
                    for kc in range(2):
                        mm(pu.ap(), wuc.ap()[:, oc, kc * 128:(kc + 1) * 128], ppr.ap()[:, kc, :], kc == 0, kc == 1,
                           [f"wuc{oc}", f"ppr{kc}"], [pu.name])
                    tt("dve", fsb.ap()[:, oc, :], sgm.ap()[:, oc, :], pu.ap(), ALU.mult, [f"sgm{oc}", pu.name], [f"fsb{oc}"])
                rms_rstd(lambda kc: fsb.ap()[:, kc, :], KC, D, sq, ps[6], tm0, tm1.ap(),
                         [f"fsb{kc}" for kc in range(KC)], "tm1")
                for oc in range(KC):
                    stt(fsb.ap()[:, oc, :], fsb.ap()[:, oc, :], gcol(l, 6, oc), tm1.ap(), ALU.mult, ALU.mult,
                        [f"fsb{oc}", "tm1", "gn"], [f"fsb{oc}"])
                    tt("dve", hx.ap()[:, oc, :], hx.ap()[:, oc, :], fsb.ap()[:, oc, :], ALU.add,
                       [f"hx{oc}", f"fsb{oc}"], [f"hx{oc}"])
                    dma(hTv[:, oc, sl], hx.ap()[:, oc, :], [f"hx{oc}"], ["hT"], "hst")
            P.barrier()
            P.emit()

    def ext_in(name, shape):
        t = nc.dram_tensor(name, list(shape), F32, kind="ExternalInput").ap()
        declared[name] = list(shape)
        return t

    def ext_out(name, shape):
        return nc.dram_tensor(name, list(shape), F32, kind="ExternalOutput").ap()

    def hcopy(dst, src):
        dv = dst.rearrange("(kc p) t -> p kc t", p=128)
        sv = src.rearrange("(kc p) t -> p kc t", p=128)
        for kc in range(KC):
            dma(dv[:, kc, :], sv[:, kc, :], [], ["hT"], "h0")
        return dv

    if phasetest:
        r_hffn1 = ext_in("r_hffn1", [D, S]); r_qkv = ext_in("r_qkv", [1536, S]); r_uP = ext_in("r_uP", [512, 8, 512])
        r_attn = ext_in("r_attn", [512, S]); r_ysc = ext_in("r_ysc", [512, 8, 512])
        r_hmix = ext_in("r_hmix", [D, S]); r_hffn2 = ext_in("r_hffn2", [D, S])
        o_h2 = ext_out("o_h2", [D, S]); o_h3 = ext_out("o_h3", [D, S]); o_h4 = ext_out("o_h4", [D, S])
        import os as _os3
        sel = _os3.environ.get("KSEL", "01234567")
        if "0" in sel:
            ffn_phase(0, 0)
        hTv = r_hffn1.rearrange("(kc p) t -> p kc t", p=128)
        if "1" in sel:
            inproj_phase(0)
        if "2" in sel:
            rope_phase()
        qkvT_save = qkvT
        qkvT = r_qkv
        if "3" in sel:
            attn_phase()
        qkvT = qkvT_save
        uP_save = uP
        uP = r_uP
        if "4" in sel:
            s5_phase(0)
        uP = uP_save
        attnT_src, ysc_src = r_attn, r_ysc
        if "5" in sel:
            hTv = hcopy(o_h2, r_hffn1)
            mixout_phase(0)
        if "6" in sel:
            hTv = hcopy(o_h3, r_hmix)
            ffn_phase(0, 1)
        if "7" in sel:
            hTv = hcopy(o_h4, r_hffn2)
            ple_phase(0)
    else:
        for l in range(n_layers):
            ffn_phase(l, 0)
            inproj_phase(l)
            rope_phase()
            attn_phase()
            s5_phase(l)
            mixout_phase(l)
            ffn_phase(l, 1)
            ple_phase(l)
    P.barrier()
    P.emit(final_wait=True)
    st_all.close()
    nc_declared[id(nc)] = dict(declared)
    return nc


def host_inputs(inp):
    f = np.float32
    g = lambda k: np.asarray(inp[k])
    sh = {}
    sh["wg1"] = slab_layout(g("ffn1_w_gate")); sh["wu1"] = slab_layout(g("ffn1_w_up"))
    sh["wg2"] = slab_layout(g("ffn2_w_gate")); sh["wu2"] = slab_layout(g("ffn2_w_up"))
    sh["wd1"] = np.ascontiguousarray(g("ffn1_w_down").reshape(NL, FC, 128, D))
    sh["wd2"] = np.ascontiguousarray(g("ffn2_w_down").reshape(NL, FC, 128, D))
    sh["win"] = slab_layout(g("w_in"))
    sh["wout"] = slab_layout(g("w_out"))
    sh["wglu"] = slab_layout(g("ssm_w_glu"))
    sh["plewu"] = slab_layout(g("ple_w_up"))
    sh["plewg"] = slab_layout(g("ple_w_gate"))
    names = ["ffn1_pre_g", "ffn1_post_g", "mix_pre_g", "mix_post_g", "ffn2_pre_g", "ffn2_post_g", "ple_post_g"]
    gs = np.stack([g(n) for n in names], axis=1)
    sh["gains"] = np.ascontiguousarray(gs.reshape(NL, 7, 8, 128).transpose(3, 0, 1, 2).reshape(128, NL * 7 * 8))
    g2 = np.stack([g("attn_norm_g"), g("ssm_norm_g")], axis=1)
    sh["gains2"] = np.ascontiguousarray(g2.reshape(NL, 2, 4, 128).transpose(3, 0, 1, 2).reshape(128, NL * 2 * 4))
    sh["bglu"] = np.ascontiguousarray(g("ssm_b_glu").reshape(NL, 4, 128).transpose(2, 0, 1).reshape(128, NL * 4))
    dup = lambda a: np.ascontiguousarray(np.concatenate([a, a], axis=1))
    sh["s_lr"] = dup(g("ssm_lam_re").transpose(0, 2, 1))
    sh["s_li"] = dup(g("ssm_lam_im").transpose(0, 2, 1))
    sh["s_ld"] = np.ascontiguousarray(np.broadcast_to(g("ssm_log_dt")[:, None, :], (NL, 128, 32)))
    br = g("ssm_b_re").transpose(0, 2, 1, 3).reshape(NL, 64, 512)
    bi = g("ssm_b_im").transpose(0, 2, 1, 3).reshape(NL, 64, 512)
    sh["s_bx1"] = np.ascontiguousarray(np.concatenate([br, bi], axis=1))
    sh["s_bx2"] = np.ascontiguousarray(np.concatenate([bi, br], axis=1))
    cr = g("ssm_c_re").transpose(0, 3, 1, 2).reshape(NL, 64, 512)
    ci = g("ssm_c_im").transpose(0, 3, 1, 2).reshape(NL, 64, 512)
    sh["s_cy1"] = np.ascontiguousarray(np.concatenate([cr, ci], axis=1))
    sh["s_cy2"] = np.ascontiguousarray(np.concatenate([ci, cr], axis=1))
    dd = g("ssm_d").transpose(0, 2, 1)
    sh["s_dst"] = np.ascontiguousarray(np.tile(dd, (1, 8, 1)))
    sh["c_ident"] = np.eye(128, dtype=f)
    jj = np.arange(128) // 16
    sh["c_causal"] = (jj[None, :] >= jj[:, None]).astype(f)
    k_ = np.arange(128)[:, None]; q_ = np.arange(128)[None, :]
    sh["c_mask"] = np.concatenate([(k_ <= q_), (k_ >= q_)], axis=1).astype(f)
    shf = np.zeros((128, 64), f); shf[64 + np.arange(64), np.arange(64)] = 1.0
    sh["c_shift"] = shf
    cv = np.zeros((128, 8), f)
    cv[:64, 0] = -1; cv[64:, 0] = 1
    cv[:64, 1] = 1; cv[64:, 1] = -1
    invf = (np.float32(500000.0) ** (-np.arange(8, dtype=np.float32) * np.float32(2.0 / 16))).astype(f)
    d16 = np.arange(128) % 16
    cv[:, 2] = invf[d16 % 8]
    cv[:, 3] = np.where(d16 < 8, -1.0, 1.0)
    sh["c_vec"] = cv
    ex = np.array([7, 6, 5, 4, 3, 2, 1, 0, 1, 2, 3, 4, 5, 6, 7, 8, -7, -6, -5, -4, -3, -2, -1, 0], f)
    sh["c_expo"] = np.ascontiguousarray(np.broadcast_to(ex[None, :], (128, 24)))
    return sh


_CACHE = {}
nc_declared = {}


def kernel(**inputs):
    if "nc" not in _CACHE:
        _CACHE["nc"] = build_program()
    nc = _CACHE["nc"]
    sh = host_inputs(inputs)
    x = np.asarray(inputs["x"]); p = np.asarray(inputs["p"]); positions = np.asarray(inputs["positions"])
    in_maps = []
    for b in range(8):
        m = dict(sh)
        m["xT"] = np.ascontiguousarray(x[b].T)
        m["pT"] = np.ascontiguousarray(p[:, b].transpose(0, 2, 1))
        m["pos"] = np.ascontiguousarray(positions[b:b + 1].astype(np.int32))
        in_maps.append({k: v for k, v in m.items() if k in nc_declared[id(nc)]})
    res = run_bass_kernel_spmd(nc, in_maps, core_ids=list(range(8)))
    outs = [np.asarray(r["out"]).T for r in res.results]
    return np.ascontiguousarray(np.stack(outs, axis=0).astype(np.float32))
```
